# Optimizing a Trainium2 kernel written in Bass

```python
import math
import jax, jax.numpy as jnp
from jax import lax
import numpy as np

D_MODEL = 2048
BATCH = 2
SEQ = 16384
DEPTH = 2

N_EVEN = (DEPTH + 1) // 2
N_ODD = DEPTH // 2
MEM_LEN = 256
MAX_POS_OFFSET = 4096

ALPHA = (2 * DEPTH) ** 0.25
BETA = (8 * DEPTH) ** -0.25
LN_EPS = 1e-5

ROPE_THETA = 500000.0
ROPE_FRAC = 4
Q_BLOCK = 128

DIFF_HEADS = 8
DIFF_HEAD_DIM = 64
DIFF_V_DIM = 2 * DIFF_HEAD_DIM
DIFF_WIDTH = DIFF_HEADS * DIFF_V_DIM

CONV_CH = D_MODEL - DIFF_WIDTH
CONV_WIDTH = 31

EVEN_IN = 3 * DIFF_WIDTH + 2 * CONV_CH
EVEN_SPLITS = (DIFF_WIDTH, 2 * DIFF_WIDTH, 3 * DIFF_WIDTH, 3 * DIFF_WIDTH + CONV_CH)

DSA_HEADS = 16
DSA_KV_HEADS = 4
DSA_HEAD_DIM = 128
IDX_HEADS = 4
IDX_DIM = 64
TOPK_MAX = 256
DSA_Q = DSA_HEADS * DSA_HEAD_DIM
DSA_KV = DSA_KV_HEADS * DSA_HEAD_DIM
IDX_Q = IDX_HEADS * IDX_DIM
ODD_IN = DSA_Q + 2 * DSA_KV + IDX_Q + IDX_DIM + IDX_HEADS
ODD_SPLITS = (DSA_Q, DSA_Q + DSA_KV, DSA_Q + 2 * DSA_KV,
              DSA_Q + 2 * DSA_KV + IDX_Q, DSA_Q + 2 * DSA_KV + IDX_Q + IDX_DIM)

XA_HEADS = 4
XA_HEAD_DIM = D_MODEL // XA_HEADS

D_FF = -(-(8 * D_MODEL) // (3 * 256)) * 256

kernel_name = 'hybrid_diffattn_conformer_dsa_deepnorm'


def layer_norm(x, g, b):
    xf = x.astype(jnp.float32)
    mu = jnp.mean(xf, axis=-1, keepdims=True)
    var = jnp.mean(jnp.square(xf - mu), axis=-1, keepdims=True)
    return ((xf - mu) * lax.rsqrt(var + LN_EPS) * g + b).astype(x.dtype)


def rms_norm(x, g):
    xf = x.astype(jnp.float32)
    y = xf * lax.rsqrt(jnp.mean(xf * xf, axis=-1, keepdims=True) + LN_EPS)
    return (y * g).astype(x.dtype)


def partial_rotary(x, positions):
    dh = x.shape[-1]
    rot = dh // ROPE_FRAC
    half = rot // 2
    inv_freq = 1.0 / (ROPE_THETA ** (jnp.arange(half, dtype=jnp.float32) / half))
    ang = positions.astype(jnp.float32)[..., None] * inv_freq
    cos = jnp.cos(ang)[:, :, None, :]
    sin = jnp.sin(ang)[:, :, None, :]
    xr = x[..., :rot].astype(jnp.float32)
    x1, x2 = xr[..., :half], xr[..., half:]
    rotated = jnp.concatenate([x1 * cos - x2 * sin, x2 * cos + x1 * sin], axis=-1)
    return jnp.concatenate([rotated.astype(x.dtype), x[..., rot:]], axis=-1)


def diff_attention(q, k, v, lam):
    B, S, H, _, d = q.shape
    nb = S // Q_BLOCK
    scale = d ** -0.5
    qb = q.reshape(B, nb, Q_BLOCK, H, 2, d).transpose(1, 0, 2, 3, 4, 5)
    kpos = jnp.arange(S)

    def block(args):
        q_blk, i = args
        qpos = i * Q_BLOCK + jnp.arange(Q_BLOCK)
        causal = kpos[None, :] <= qpos[:, None]
        s = jnp.einsum('bqhcd,bkhcd->bhcqk', q_blk, k).astype(jnp.float32) * scale
        p = jax.nn.softmax(jnp.where(causal, s, -jnp.inf), axis=-1)
        p_diff = p[:, :, 0] - lam * p[:, :, 1]
        return jnp.einsum('bhqk,bkhe->bqhe', p_diff.astype(v.dtype), v)

    out = lax.map(block, (qb, jnp.arange(nb)))
    return out.transpose(1, 0, 2, 3, 4).reshape(B, S, H, 2 * d)


def even_mixer(x, positions, w_in, w_out, lam_p, subln_g, conv_w, conv_b,
               conv_ln_g, conv_ln_b, lam_init):
    B, S, _ = x.shape
    q, k, v, glu_val, glu_gate = jnp.split(x @ w_in, EVEN_SPLITS, axis=-1)
    q = partial_rotary(q.reshape(B, S, 2 * DIFF_HEADS, DIFF_HEAD_DIM), positions)
    k = partial_rotary(k.reshape(B, S, 2 * DIFF_HEADS, DIFF_HEAD_DIM), positions)
    q = q.reshape(B, S, DIFF_HEADS, 2, DIFF_HEAD_DIM)
    k = k.reshape(B, S, DIFF_HEADS, 2, DIFF_HEAD_DIM)
    v = v.reshape(B, S, DIFF_HEADS, DIFF_V_DIM)
    lp = lam_p.astype(jnp.float32)
    lam = (jnp.exp(jnp.sum(lp[0] * lp[1])) - jnp.exp(jnp.sum(lp[2] * lp[3])) + lam_init)
    a = diff_attention(q, k, v, lam)
    a = (rms_norm(a, subln_g) * (1.0 - lam_init)).reshape(B, S, DIFF_WIDTH)
    u = glu_val * jax.nn.sigmoid(glu_gate)
    c = lax.conv_general_dilated(
        u, conv_w[:, None, :].astype(u.dtype), window_strides=(1,),
        padding=[(CONV_WIDTH - 1, 0)], dimension_numbers=('NWC', 'WIO', 'NWC'),
        feature_group_count=CONV_CH) + conv_b
    c = jax.nn.silu(layer_norm(c, conv_ln_g, conv_ln_b))
    return jnp.concatenate([a, c], axis=-1) @ w_out


def odd_mixer(x, positions, w_in, w_out):
    B, S, _ = x.shape
    q, k, v, qi, ki, wi = jnp.split(x @ w_in, ODD_SPLITS, axis=-1)
    q = partial_rotary(q.reshape(B, S, DSA_HEADS, DSA_HEAD_DIM), positions)
    k = partial_rotary(k.reshape(B, S, DSA_KV_HEADS, DSA_HEAD_DIM), positions)
    v = v.reshape(B, S, DSA_KV_HEADS, DSA_HEAD_DIM)
    qi = partial_rotary(qi.reshape(B, S, IDX_HEADS, IDX_DIM), positions)
    ki = partial_rotary(ki.reshape(B, S, 1, IDX_DIM), positions)[:, :, 0]
    wi = wi * (IDX_HEADS ** -0.5 * IDX_DIM ** -0.5)
    topk = min(TOPK_MAX, S // 4)
    rep = DSA_HEADS // DSA_KV_HEADS
    scale = DSA_HEAD_DIM ** -0.5
    nb = S // Q_BLOCK
    qb = q.reshape(B, nb, Q_BLOCK, DSA_KV_HEADS, rep, DSA_HEAD_DIM).transpose(1, 0, 2, 3, 4, 5)
    qib = qi.reshape(B, nb, Q_BLOCK, IDX_HEADS, IDX_DIM).transpose(1, 0, 2, 3, 4)
    wib = wi.reshape(B, nb, Q_BLOCK, IDX_HEADS).transpose(1, 0, 2, 3)
    kpos = jnp.arange(S)
    gather = jax.vmap(lambda kb, ib: kb[ib])

    def block(args):
        q_blk, qi_blk, w_blk, i = args
        qpos = i * Q_BLOCK + jnp.arange(Q_BLOCK)
        causal = kpos[None, :] <= qpos[:, None]
        rel = jax.nn.relu(jnp.einsum('bqhd,bkd->bqhk', qi_blk, ki).astype(jnp.float32))
        score = jnp.einsum('bqh,bqhk->bqk', w_blk.astype(jnp.float32), rel)
        score = jnp.where(causal[None], score, -jnp.inf)
        _, sel = lax.top_k(score, topk)
        valid = sel <= qpos[None, :, None]
        k_sel = gather(k, sel)
        v_sel = gather(v, sel)
        s = jnp.einsum('bqgrd,bqkgd->bgrqk', q_blk, k_sel).astype(jnp.float32) * scale
        p = jax.nn.softmax(jnp.where(valid[:, None, None], s, -jnp.inf), axis=-1)
        return jnp.einsum('bgrqk,bqkgd->bqgrd', p.astype(v_sel.dtype), v_sel)

    out = lax.map(block, (qb, qib, wib, jnp.arange(nb)))
    out = out.transpose(1, 0, 2, 3, 4, 5).reshape(B, S, DSA_Q)
    return out @ w_out


def memory_cross_attention(x, mem, wq, wkv, wo):
    B, S, _ = x.shape
    q = (x @ wq).reshape(B, S, XA_HEADS, XA_HEAD_DIM)
    k, v = jnp.split(mem @ wkv, 2, axis=-1)
    k = k.reshape(B, -1, XA_HEADS, XA_HEAD_DIM)
    v = v.reshape(B, -1, XA_HEADS, XA_HEAD_DIM)
    s = jnp.einsum('bqhd,bmhd->bhqm', q, k).astype(jnp.float32) * XA_HEAD_DIM ** -0.5
    p = jax.nn.softmax(s, axis=-1).astype(v.dtype)
    o = jnp.einsum('bhqm,bmhd->bqhd', p, v).reshape(B, S, D_MODEL)
    return o @ wo


def swiglu_ffn(x, w_in, w_out):
    gate, up = jnp.split(x @ w_in, 2, axis=-1)
    return (jax.nn.silu(gate) * up) @ w_out


def _normal(key, shape, scale):
    return jax.random.normal(key, shape, jnp.float32) * scale


def setup_inputs(seed: int = 0) -> dict:
    key = jax.random.key(seed)
    ks = jax.random.split(key, 24)
    D = D_MODEL
    x = _normal(ks[0], (BATCH, SEQ, D), 1.0)
    mem = _normal(ks[1], (BATCH, MEM_LEN, D), 1.0)
    positions = (jax.random.randint(ks[2], (BATCH, 1), 0, MAX_POS_OFFSET, dtype=jnp.int32)
                 + jnp.arange(SEQ, dtype=jnp.int32)[None, :])
    even_cols = jnp.concatenate([jnp.ones((2 * DIFF_WIDTH,), jnp.float32),
                                 jnp.full((DIFF_WIDTH,), BETA, jnp.float32),
                                 jnp.ones((2 * CONV_CH,), jnp.float32)])
    w_in_even = _normal(ks[3], (N_EVEN, D, EVEN_IN), D ** -0.5) * even_cols
    w_out_even = _normal(ks[4], (N_EVEN, D, D), D ** -0.5 * BETA)
    diff_lambda = _normal(ks[5], (N_EVEN, 4, DIFF_HEAD_DIM), 0.1)
    diff_subln_g = 1.0 + _normal(ks[6], (N_EVEN, DIFF_V_DIM), 0.02)
    conv_w = _normal(ks[7], (N_EVEN, CONV_WIDTH, CONV_CH), CONV_WIDTH ** -0.5)
    conv_b = _normal(ks[8], (N_EVEN, CONV_CH), 0.01)
    conv_ln_g = 1.0 + _normal(ks[9], (N_EVEN, CONV_CH), 0.02)
    conv_ln_b = _normal(ks[10], (N_EVEN, CONV_CH), 0.02)
    odd_cols = jnp.concatenate([jnp.ones((DSA_Q + DSA_KV,), jnp.float32),
                                jnp.full((DSA_KV,), BETA, jnp.float32),
                                jnp.ones((IDX_Q + IDX_DIM + IDX_HEADS,), jnp.float32)])
    w_in_odd = _normal(ks[11], (N_ODD, D, ODD_IN), D ** -0.5) * odd_cols
    w_out_odd = _normal(ks[12], (N_ODD, DSA_Q, D), DSA_Q ** -0.5 * BETA)
    xa_wq = _normal(ks[13], (DEPTH, D, D), D ** -0.5)
    xa_cols = jnp.concatenate([jnp.ones((D,), jnp.float32), jnp.full((D,), BETA, jnp.float32)])
    xa_wkv = _normal(ks[14], (DEPTH, D, 2 * D), D ** -0.5) * xa_cols
    xa_wo = _normal(ks[15], (DEPTH, D, D), D ** -0.5 * BETA)
    ffn_w_in = _normal(ks[16], (DEPTH, D, 2 * D_FF), D ** -0.5 * BETA)
    ffn_w_out = _normal(ks[17], (DEPTH, D_FF, D), D_FF ** -0.5 * BETA)
    ln_g = 1.0 + _normal(ks[18], (DEPTH, 3, D), 0.02)
    ln_b = _normal(ks[19], (DEPTH, 3, D), 0.02)
    return {'x': x, 'mem': mem, 'positions': positions,
            'w_in_even': w_in_even, 'w_out_even': w_out_even,
            'diff_lambda': diff_lambda, 'diff_subln_g': diff_subln_g,
            'conv_w': conv_w, 'conv_b': conv_b, 'conv_ln_g': conv_ln_g, 'conv_ln_b': conv_ln_b,
            'w_in_odd': w_in_odd, 'w_out_odd': w_out_odd,
            'xa_wq': xa_wq, 'xa_wkv': xa_wkv, 'xa_wo': xa_wo,
            'ffn_w_in': ffn_w_in, 'ffn_w_out': ffn_w_out,
            'ln_g': ln_g, 'ln_b': ln_b}


def reference(x, mem, positions, w_in_even, w_out_even, diff_lambda, diff_subln_g,
              conv_w, conv_b, conv_ln_g, conv_ln_b, w_in_odd, w_out_odd,
              xa_wq, xa_wkv, xa_wo, ffn_w_in, ffn_w_out, ln_g, ln_b):
    for layer in range(DEPTH):
        j = layer // 2
        if layer % 2 == 0:
            lam_init = 0.8 - 0.6 * math.exp(-0.3 * layer)
            h = even_mixer(x, positions, w_in_even[j], w_out_even[j], diff_lambda[j],
                           diff_subln_g[j], conv_w[j], conv_b[j], conv_ln_g[j],
                           conv_ln_b[j], lam_init)
        else:
            h = odd_mixer(x, positions, w_in_odd[j], w_out_odd[j])
        x = layer_norm(ALPHA * x + h, ln_g[layer, 0], ln_b[layer, 0])
        h = memory_cross_attention(x, mem, xa_wq[layer], xa_wkv[layer], xa_wo[layer])
        x = layer_norm(ALPHA * x + h, ln_g[layer, 1], ln_b[layer, 1])
        h = swiglu_ffn(x, ffn_w_in[layer], ffn_w_out[layer])
        x = layer_norm(ALPHA * x + h, ln_g[layer, 2], ln_b[layer, 2])
    return x
```

```python
import math
from contextlib import ExitStack

import numpy as np

import concourse.bass as bass
import concourse.mybir as mybir
from concourse.bass_utils import run_bass_kernel_spmd

F32 = mybir.dt.float32
BF16 = mybir.dt.bfloat16
I32 = mybir.dt.int32
AF = mybir.ActivationFunctionType
ALU = mybir.AluOpType
AX = mybir.AxisListType

P = 128
N = 512
D = 2048
KC = 16
HALO = 32
DEPTH = 2
ALPHA = (2 * DEPTH) ** 0.25
LN_EPS = 1e-5
ROPE_THETA = 500000.0
D_FF = 5632
FC = D_FF // 128
TWO_PI = 2.0 * math.pi
MAGIC = 12582912.0
NEG_BIG = -1.0e30
TIE_EPS = 2.0 ** -100

V_LNG = 0
V_LNB = 48
V_CB = 96
V_CLG = 104
V_CLB = 112
V_CW = 120
V_SUBG = 368
V_INVF64 = 369
V_INVF128 = 370
NVEC = 372


class Eng:
    def __init__(self, eng, sem, is_pe=False):
        self.eng = eng
        self.sem = sem
        self.n = 0
        self.seen = {}
        self.is_pe = is_pe


class Ctx:
    def __init__(self, nc, st):
        self.nc = nc
        self.st = st
        self.pe = Eng(nc.tensor, st.enter_context(nc.semaphore("s_pe")), True)
        self.act = Eng(nc.scalar, st.enter_context(nc.semaphore("s_act")))
        self.dve = Eng(nc.vector, st.enter_context(nc.semaphore("s_dve")))
        self.pool = Eng(nc.gpsimd, st.enter_context(nc.semaphore("s_pool")))
        self.sp = Eng(nc.sync, st.enter_context(nc.semaphore("s_sp")))
        self.engs = [self.pe, self.act, self.dve, self.pool, self.sp]
        self.lastw = {}
        self.rd = {}
        self.dsems = {}
        self.dcnt = {}
        self.nsem = 0

    def dsem(self, name):
        s = self.st.enter_context(self.nc.semaphore("d_" + name))
        self.dsems[s.num] = s
        self.dcnt[s.num] = 0
        return s

    def _wait(self, e, reads, writes):
        need = {}

        def add(tok):
            num = tok[0].num
            if num not in need or need[num][1] < tok[1]:
                need[num] = tok

        for k in reads:
            w = self.lastw.get(k)
            if w is not None:
                add(w)
        for k in writes:
            w = self.lastw.get(k)
            if w is not None:
                add(w)
            for tok in self.rd.get(k, {}).values():
                add(tok)
        for num, (sem, val) in need.items():
            if e.is_pe and sem is e.sem:
                continue
            if e.seen.get(num, 0) >= val:
                continue
            e.eng.wait_ge(sem, val)
            e.seen[num] = val

    def _post(self, tok, reads, writes):
        for k in reads:
            self.rd.setdefault(k, {})[tok[0].num] = tok
        for k in writes:
            self.lastw[k] = tok
            self.rd[k] = {}

    def op(self, e, fn, reads=(), writes=()):
        pbr = [k for k in reads if isinstance(k, tuple) and k and k[0] == "pb"]
        if pbr:
            reads = [k for k in reads if k not in pbr]
            writes = list(writes) + pbr
        self._wait(e, reads, writes)
        ins = fn()
        ins.then_inc(e.sem, 1)
        e.n += 1
        tok = (e.sem, e.n)
        self._post(tok, reads, writes)
        return tok

    def dma(self, q, sem, out, in_, reads=(), writes=()):
        return self.dmas(q, sem, [(out, in_)], reads, writes)

    def dmas(self, q, sem, pairs, reads=(), writes=()):
        self._wait(q, reads, writes)
        for out, in_ in pairs:
            ins = q.eng.dma_start(out=out, in_=in_)
            ins.then_inc(sem, 16)
            self.dcnt[sem.num] += 16
        tok = (sem, self.dcnt[sem.num])
        self._post(tok, reads, writes)
        return tok

    def barrier(self):
        toks = [(e.sem, e.n) for e in self.engs if e.n > 0]
        toks += [(self.dsems[n], c) for n, c in self.dcnt.items() if c > 0]
        for e in self.engs:
            for sem, val in toks:
                if sem is e.sem:
                    continue
                if e.seen.get(sem.num, 0) >= val:
                    continue
                e.eng.wait_ge(sem, val)
                e.seen[sem.num] = val
        self.lastw = {}
        self.rd = {}


_UID = [0]
STOP = [99, 9, 9]
_NC_CACHE = {}


class Rot:
    def __init__(self, cx, st, name, shape, dt, n, sems=None):
        _UID[0] += 1
        self.tiles = [st.enter_context(cx.nc.sbuf_tensor(f"{name}{i}_{_UID[0]}", list(shape), dt)) for i in range(n)]
        self.keys = [(name, i) for i in range(n)]
        self.sems = sems
        self.i = 0

    def next_s(self):
        sm = self.sems[self.i]
        t, k = self.next()
        return t, k, sm

    def next(self):
        t, k = self.tiles[self.i], self.keys[self.i]
        self.i = (self.i + 1) % len(self.tiles)
        return t, k


class WStream:
    def __init__(self, cx, st, nslots, slot_elems, sems):
        self.cx = cx
        _UID[0] += 1
        self.slots = [st.enter_context(cx.nc.sbuf_tensor(f"wslot{i}_{_UID[0]}", [P, slot_elems], BF16)) for i in range(nslots)]
        self.sems = sems
        self.reqs = []
        self.issued = 0
        self.popped = 0
        self.released = 0
        self.nslots = nslots

    def push(self, w_ap, r0, kch, c0, ncols):
        self.reqs.append((w_ap, r0, kch, c0, ncols))
        self._issue()

    def _issue(self):
        while self.issued < len(self.reqs) and self.issued < self.released + self.nslots:
            i = self.issued
            w_ap, r0, kch, c0, ncols = self.reqs[i]
            s = i % self.nslots
            dst = self.slots[s][:, 0:kch * ncols].rearrange("p (k n) -> p k n", k=kch)
            src = w_ap[r0:r0 + kch * P, c0:c0 + ncols].rearrange("(k p) n -> p k n", p=P)
            self.cx.dma(self.cx.pool, self.sems[s], dst, src, writes=[("wslot", s)])
            self.issued += 1

    def pop(self):
        i = self.popped
        assert i < self.issued, "weight stream underflow"
        self.popped += 1
        w_ap, r0, kch, c0, ncols = self.reqs[i]
        s = i % self.nslots
        view = self.slots[s][:, 0:kch * ncols].rearrange("p (k n) -> p k n", k=kch)
        return view, ("wslot", s)

    def release(self):
        self.released += 1
        self._issue()


class Builder:
    def __init__(self, layer, NG, dbg=False):
        self.layer = layer
        self.NG = NG
        self.S = 2048 * NG
        self.dbg = dbg
        self.nc = bass.Bass("TRN2", target_bir_lowering=False)
        self.outs = []

    def din(self, name, shape, dt=F32):
        return self.nc.dram_tensor(name, list(shape), dt, kind="ExternalInput").ap()

    def dout(self, name, shape, dt=F32):
        self.outs.append(name)
        return self.nc.dram_tensor(name, list(shape), dt, kind="ExternalOutput").ap()

    def dscr(self, name, shape, dt):
        return self.nc.dram_tensor(name, list(shape), dt).ap()

    def sb(self, st, name, shape, dt):
        _UID[0] += 1
        return st.enter_context(self.nc.sbuf_tensor(f"{name}_{_UID[0]}", list(shape), dt))

    def bank(self):
        i = self.bank_i
        self.bank_i = (self.bank_i + 1) % self.nrot
        return self.pb[i], ("pb", i)

    def build(self):
        nc = self.nc
        NG, S, layer = self.NG, self.S, self.layer
        with ExitStack() as st:
            self.st = st
            cx = self.cx = Ctx(nc, st)
            self.declare_io()
            self.pb = [st.enter_context(nc.psum_tensor(f"pb{i}", [P, N], F32)) for i in range(8)]
            self.bank_i = 0
            self.nrot = 8
            self.setup_consts(st)
            stop = getattr(self, "stop", 99)
            if stop >= 1:
                if layer == 0:
                    self.phaseA_even()
                else:
                    self.phaseA_odd()
            self.wsems = [cx.dsem(f"w{i}") for i in range(4)]
            self.KmT = self.sb(st, "KmT", [P, KC, 256], BF16)
            self.Vm = self.sb(st, "Vm", [P, 2, D], BF16)
            if layer == 0:
                self.ws = WStream(cx, st, 4, 16 * 256, self.wsems)
                self.xT32 = self.sb(st, "xT32", [P, KC, N], F32)
                self.xTb = self.sb(st, "xTb", [P, KC, HALO + N], BF16)
                if stop >= 2:
                    self.prologue_xa()
                if stop >= 3:
                    for m in range(NG):
                        self.group(m)
            else:
                with ExitStack() as stp:
                    self.ws = WStream(cx, stp, 4, 16 * 256, self.wsems)
                    if stop >= 2:
                        self.prologue_xa()
                if stop >= 3:
                    for m in range(NG):
                        self.group_odd(m)
            cx.barrier()
        return nc

    def declare_io(self):
        NG, S, layer = self.NG, self.S, self.layer
        d = self.din
        self.xb = d("xb", [S, D])
        self.posb = d("posb", [1, S], I32)
        self.xo = d("xo", [NG, 4, P, D])
        self.poso = d("poso", [1, NG * N], I32)
        self.vec = d("vec", [P, NVEC])
        self.cmat = d("cmat", [P, 3 * P])
        self.cmask = d("cmask", [16, P, N])
        self.memb = d("memb", [256, D])
        self.w_in = d("w_in", [D, 5120 if layer == 0 else 3396])
        self.w_out = d("w_out", [D, D])
        self.xa_wq = d("xa_wq", [D, D])
        self.xa_wkv = d("xa_wkv", [D, 2 * D])
        self.xa_wo = d("xa_wo", [D, D])
        self.ffn_w_in = d("ffn_w_in", [D, 2 * D_FF])
        self.ffn_w_out = d("ffn_w_out", [D_FF, D])
        if layer == 0:
            self.xh = d("xh", [NG, HALO, D])
            self.lam_in = d("lam_in", [1, 256])
            self.KT = self.dscr("KT", [8, P, S], BF16)
            self.Vs = self.dscr("Vs", [S, 1024], BF16)
        else:
            self.tb0_in = d("tb0", [P, N])
            self.KT = self.dscr("KT", [4, P, S], BF16)
            self.Vs = self.dscr("Vs", [S, 512], BF16)
        self.out = self.dout("out", [NG, 4, P, D])
        if self.dbg:
            self.dbg1 = self.dout("dbg1", [NG, 4, P, D])
            self.dbg2 = self.dout("dbg2", [NG, 4, P, D])

    def setup_consts(self, st):
        cx, nc = self.cx, self.nc
        sb = self.sb
        self.vecs = sb(st, "vecs", [P, NVEC], F32)
        self.cm32 = sb(st, "cm32", [P, 3 * P], F32)
        self.cmb = sb(st, "cmb", [P, 3 * P], BF16)
        self.ones_b = sb(st, "ones_b", [P, P], BF16)
        self.ones32 = sb(st, "ones32", [P, P], F32)
        self.onesc32 = sb(st, "onesc32", [P, P], F32)
        self.onesh32 = sb(st, "onesh32", [P, P], F32)
        s0 = cx.dsem("const")
        cx.dmas(cx.sp, s0, [(self.vecs[:], self.vec), (self.cm32[:], self.cmat)], writes=["vecs", "cm32"])
        cx.op(cx.dve, lambda: nc.vector.tensor_copy(out=self.cmb[:], in_=self.cm32[:]), reads=["cm32"], writes=["cmb"])
        cx.op(cx.dve, lambda: nc.vector.memset(self.ones_b[:], 1.0), writes=["ones_b"])
        cx.op(cx.dve, lambda: nc.vector.memset(self.ones32[:], 1.0 / D), writes=["ones32"])
        cx.op(cx.dve, lambda: nc.vector.memset(self.onesc32[:], 1.0 / 1024), writes=["onesc32"])
        cx.op(cx.dve, lambda: nc.vector.memset(self.onesh32[:], 1.0 / 128), writes=["onesh32"])
        self.ident32 = self.cm32[:, 0:P]
        self.R64b = self.cmb[:, P:2 * P]
        self.R128b = self.cmb[:, 2 * P:3 * P]
        self.sem_cmask = cx.dsem("cmask")
        if self.layer == 0:
            self.cmk = sb(st, "cmk", [P, 16, N], BF16)
            cx.dma(cx.pool, self.sem_cmask, self.cmk[:], self.cmask.rearrange("k p n -> p k n"), writes=["cmk"])
        self.kmx = sb(st, "kmx", [P, 8], F32)
        self.xkmx = sb(st, "xkmx", [P, 4], F32)
        self.sem_x = cx.dsem("x")
        self.sem_kv = [cx.dsem("kv0"), cx.dsem("kv1")]
        self.sem_st = cx.dsem("st")
        self.sem_pos = cx.dsem("pos")
        self.sem_pos2 = cx.dsem("pos2")
        self.sem_halo = cx.dsem("halo")
        self.sem_lp = cx.dsem("lp")
        self.sem_rot = [cx.dsem(f"rot{i}") for i in range(8)]

    def rope_tables(self, st, pos_ap, n, invf_col, Ct, St, key, sem=None):
        cx, nc = self.cx, self.nc
        pi_ = self.sb(st, f"pos_i_{key}", [P, n], I32)
        pf = self.sb(st, f"pos_f_{key}", [P, n], F32)
        kk = self.sb(st, f"pos_k_{key}", [P, n], F32)
        rr = self.sb(st, f"pos_r_{key}", [P, n], F32)
        cx.dma(cx.sp, sem or self.sem_pos, pi_[:], pos_ap.partition_broadcast(P), writes=[("pi", key)])
        cx.op(cx.dve, lambda: nc.vector.tensor_copy(out=pf[:], in_=pi_[:]), reads=[("pi", key)], writes=[("pf", key)])
        invf = self.vecs[:, invf_col:invf_col + 1]
        cx.op(cx.dve, lambda: nc.vector.tensor_scalar(out=pf[:], in0=pf[:], scalar1=invf, scalar2=None, op0=ALU.mult),
              reads=[("pf", key), "vecs"], writes=[("pf", key)])
        for which, tab, shift in (("c", Ct, 0.25), ("s", St, 0.0)):
            cx.op(cx.dve, lambda: nc.vector.tensor_scalar(out=kk[:], in0=pf[:], scalar1=1.0 / TWO_PI, scalar2=shift,
                                                          op0=ALU.mult, op1=ALU.add),
                  reads=[("pf", key)], writes=[("kk", key)])
            cx.op(cx.dve, lambda: nc.vector.tensor_scalar(out=kk[:], in0=kk[:], scalar1=MAGIC, scalar2=None, op0=ALU.add),
                  reads=[("kk", key)], writes=[("kk", key)])
            cx.op(cx.dve, lambda: nc.vector.tensor_scalar(out=kk[:], in0=kk[:], scalar1=-MAGIC, scalar2=None, op0=ALU.add),
                  reads=[("kk", key)], writes=[("kk", key)])
            cx.op(cx.dve, lambda: nc.vector.scalar_tensor_tensor(out=rr[:], in0=kk[:], scalar=-TWO_PI, in1=pf[:], op0=ALU.mult, op1=ALU.add),
                  reads=[("kk", key), ("pf", key)], writes=[("rr", key)])
            cx.op(cx.dve, lambda: nc.vector.tensor_scalar(out=rr[:], in0=rr[:], scalar1=shift * TWO_PI, scalar2=math.pi - 1e-6,
                                                          op0=ALU.add, op1=ALU.min),
                  reads=[("rr", key)], writes=[("rr", key)])
            cx.op(cx.dve, lambda: nc.vector.tensor_scalar(out=rr[:], in0=rr[:], scalar1=-(math.pi - 1e-6), scalar2=None,
                                                          op0=ALU.max),
                  reads=[("rr", key)], writes=[("rr", key)])
            cx.op(cx.act, lambda tab=tab: nc.scalar.activation(out=tab, in_=rr[:], func=AF.Sin),
                  reads=[("rr", key)], writes=[("tab", key, which)])
        return [("tab", key, "c"), ("tab", key, "s")]

    def load_transpose(self, st, src_tiles_ap, ntt, dst32, dstb, dstb_off, key):
        cx, nc = self.cx, self.nc
        xtok = self.xtok
        cx.dma(cx.sp, self.sem_x, xtok[:, 0:ntt, :], src_tiles_ap.rearrange("t p d -> p t d"), writes=["xtok"])
        for kc in range(KC):
            pbk, bk = self.bank()
            for tt in range(ntt):
                cx.op(cx.pe, lambda tt=tt: nc.tensor.transpose(out=pbk[:, tt * P:(tt + 1) * P],
                                                               in_=xtok[:, tt, kc * P:(kc + 1) * P], identity=self.ident32),
                      reads=["xtok", "cm32"], writes=[bk])
            if dst32 is not None:
                cx.op(cx.act, lambda: nc.scalar.copy(out=dst32[:, kc, 0:ntt * P], in_=pbk[:, 0:ntt * P]),
                      reads=[bk], writes=[(key, "32", kc)])
            cx.op(cx.dve, lambda: nc.vector.tensor_copy(out=dstb[:, kc, dstb_off:dstb_off + ntt * P], in_=pbk[:, 0:ntt * P]),
                  reads=[bk], writes=[(key, "b", kc)])

    def lin_feat(self, w_ap, c0, ncols, rhs_fn, kch, evac, r0=0, ncol_tile=256, nfree=N):
        cx, nc = self.cx, self.nc
        ntiles = ncols // ncol_tile
        for t in range(ntiles):
            self.ws.push(w_ap, r0, kch, c0 + t * ncol_tile, ncol_tile)
        for t in range(ntiles):
            wt, wk = self.ws.pop()
            for j in range(ncol_tile // P):
                pbk, bk = self.bank()
                for kc in range(kch):
                    rhs, rk = rhs_fn(kc)
                    cx.op(cx.pe, lambda kc=kc, rhs=rhs: nc.tensor.matmul(pbk[:, 0:nfree], lhsT=wt[:, kc, j * P:(j + 1) * P], rhs=rhs,
                                                                         start=(kc == 0), stop=(kc == kch - 1)),
                          reads=[wk] + rk, writes=[bk])
                evac(t * (ncol_tile // P) + j, pbk, bk)
            self.ws.release()

    def sqmax(self, src_list, dst_col_ap, dst_key, running):
        cx, nc = self.cx, self.nc
        pbk, bk = self.bank()
        n = None
        for i, (ap, keys) in enumerate(src_list):
            sq, sk = self.sqr.next()
            n = ap.shape[-1]
            cx.op(cx.act, lambda ap=ap, sq=sq: nc.scalar.activation(out=sq[:, 0:n], in_=ap, func=AF.Square),
                  reads=keys, writes=[sk])
            cx.op(cx.pe, lambda sq=sq, i=i: nc.tensor.matmul(pbk[:, 0:n], lhsT=self.ones_b[:], rhs=sq[:, 0:n],
                                                             start=(i == 0), stop=(i == len(src_list) - 1)),
                  reads=[sk, "ones_b"], writes=[bk])
        if running:
            tmp, tk = self.colr.next()
            cx.op(cx.dve, lambda: nc.vector.reduce_max(out=tmp[:, 0:1], in_=pbk[:, 0:n], axis=AX.X), reads=[bk], writes=[tk])
            cx.op(cx.dve, lambda: nc.vector.tensor_tensor(out=dst_col_ap, in0=dst_col_ap, in1=tmp[:, 0:1], op=ALU.max),
                  reads=[tk, dst_key], writes=[dst_key])
        else:
            cx.op(cx.dve, lambda: nc.vector.reduce_max(out=dst_col_ap, in_=pbk[:, 0:n], axis=AX.X), reads=[bk], writes=[dst_key])

    def rotary_evac(self, pbk, bk, n, Rb, Ct, St, tabkeys, out_ap, out_keys):
        cx, nc = self.cx, self.nc
        raw, rk = self.rawr.next()
        t1, t1k = self.f32r.next()
        t2, t2k = self.f32r.next()
        sr = STOP[2]
        cx.op(cx.act, lambda: nc.scalar.copy(out=raw[:, 0:n], in_=pbk[:, 0:n]), reads=[bk], writes=[rk])
        if sr < 2:
            return
        cx.op(cx.dve, lambda: nc.vector.tensor_tensor(out=t1[:, 0:n], in0=pbk[:, 0:n], in1=Ct, op=ALU.mult),
              reads=[bk, tabkeys[0]], writes=[t1k])
        if sr < 3:
            return
        pb2, b2 = self.bank()
        cx.op(cx.pe, lambda: nc.tensor.matmul(pb2[:, 0:n], lhsT=Rb, rhs=raw[:, 0:n], start=True, stop=True),
              reads=[rk, "cmb"], writes=[b2])
        cx.op(cx.dve, lambda: nc.vector.tensor_tensor(out=t2[:, 0:n], in0=pb2[:, 0:n], in1=St, op=ALU.mult),
              reads=[b2, tabkeys[1]], writes=[t2k])
        cx.op(cx.dve, lambda: nc.vector.tensor_tensor(out=out_ap, in0=t1[:, 0:n], in1=t2[:, 0:n], op=ALU.add),
              reads=[t1k, t2k], writes=out_keys)

    def phaseA_even(self):
        cx, nc = self.cx, self.nc
        S = self.S
        with ExitStack() as st:
            sb = self.sb
            self.xtok = sb(st, "xtokA", [P, 4, D], F32)
            xTa = sb(st, "xTa", [P, KC, N], BF16)
            wk = sb(st, "wkA", [P, KC, 1024], BF16)
            wv = sb(st, "wvA", [P, KC, 1024], BF16)
            Ct = sb(st, "CtA", [P, N], F32)
            St = sb(st, "StA", [P, N], F32)
            self.rawr = Rot(cx, st, "rawA", [P, N], BF16, 2)
            self.f32r = Rot(cx, st, "f32A", [P, N], F32, 4)
            self.sqr = Rot(cx, st, "sqA", [P, N], BF16, 2)
            self.colr = Rot(cx, st, "colA", [P, 1], F32, 2)
            kout = Rot(cx, st, "koutA", [P, N], BF16, 4, sems=self.sem_rot[0:4])
            vout = Rot(cx, st, "voutA", [P, N], BF16, 4, sems=self.sem_rot[4:8])
            sw = cx.dsem("wA")
            pairs = []
            for h in range(2):
                pairs.append((wk[:, :, h * 512:(h + 1) * 512],
                              self.w_in[:, 1024 + h * 512:1024 + (h + 1) * 512].rearrange("(k p) n -> p k n", p=P)))
                pairs.append((wv[:, :, h * 512:(h + 1) * 512],
                              self.w_in[:, 2048 + h * 512:2048 + (h + 1) * 512].rearrange("(k p) n -> p k n", p=P)))
            cx.dmas(cx.pool, sw, pairs, writes=["wkA", "wvA"])
            cx.op(cx.dve, lambda: nc.vector.memset(self.kmx[:], 0.0), writes=["kmx"])
            for g in range(S // N):
                with ExitStack() as st2:
                    tabkeys = self.rope_tables(st2, self.posb[:, g * N:(g + 1) * N], N, V_INVF64, Ct[:], St[:], "A")
                    self.load_transpose(st2, self.xb[g * N:(g + 1) * N, :].rearrange("(t p) d -> t p d", p=P), 4, None, xTa, 0, "xTa")
                    sa = getattr(self, "stopA", 9)
                    for oc in range(8 if sa >= 2 else 0):
                        pbk, bk = self.bank()
                        for kc in range(KC):
                            cx.op(cx.pe, lambda kc=kc: nc.tensor.matmul(pbk[:], lhsT=wk[:, kc, oc * P:(oc + 1) * P], rhs=xTa[:, kc, :],
                                                                        start=(kc == 0), stop=(kc == KC - 1)),
                                  reads=["wkA", ("xTa", "b", kc)], writes=[bk])
                        ko, kk, ksem = kout.next_s()
                        self.rotary_evac(pbk, bk, N, self.R64b, Ct[:], St[:], tabkeys, ko[:], [kk])
                        if sa >= 3:
                            self.sqmax([(ko[:], [kk])], self.kmx[:, oc:oc + 1], "kmx", True)
                        if sa >= 4:
                            cx.dma(cx.sp, ksem, self.KT[oc, :, g * N:(g + 1) * N], ko[:], reads=[kk], writes=[("KT", oc, g)])
                    for tt in range(4 if sa >= 5 else 0):
                        for hf in range(2):
                            pbk, bk = self.bank()
                            for kc in range(KC):
                                cx.op(cx.pe, lambda kc=kc: nc.tensor.matmul(pbk[:], lhsT=xTa[:, kc, tt * P:(tt + 1) * P],
                                                                            rhs=wv[:, kc, hf * 512:(hf + 1) * 512],
                                                                            start=(kc == 0), stop=(kc == KC - 1)),
                                      reads=["wvA", ("xTa", "b", kc)], writes=[bk])
                            vo, vk, vsem = vout.next_s()
                            cx.op(cx.act, lambda: nc.scalar.copy(out=vo[:], in_=pbk[:]), reads=[bk], writes=[vk])
                            cx.dma(cx.sp, vsem, self.Vs[g * N + tt * P:g * N + (tt + 1) * P, hf * 512:(hf + 1) * 512], vo[:],
                                   reads=[vk], writes=[("Vs", g, tt, hf)])
                    cx.barrier()
            cx.barrier()

    def prologue_xa(self):
        cx, nc = self.cx, self.nc
        with ExitStack() as st:
            sb = self.sb
            self.xtok = sb(st, "xtokM", [P, 4, D], F32)
            memT = sb(st, "memT", [P, KC, 256], BF16)
            self.sqr = Rot(cx, st, "sqM", [P, N], BF16, 2)
            self.colr = Rot(cx, st, "colM", [P, 1], F32, 2)
            self.load_transpose(st, self.memb.rearrange("(t p) d -> t p d", p=P), 2, None, memT, 0, "memT")
            memkeys = [("memT", "b", kc) for kc in range(KC)]

            def evac_k(oc, pbk, bk):
                cx.op(cx.act, lambda: nc.scalar.copy(out=self.KmT[:, oc, :], in_=pbk[:, 0:256]), reads=[bk], writes=[("KmT", oc)])

            self.lin_feat(self.xa_wkv, 0, D, lambda kc: (memT[:, kc, :], [memkeys[kc]]), KC, evac_k, nfree=256)
            for hh in range(4):
                self.sqmax([(self.KmT[:, 4 * hh + dc, :], [("KmT", 4 * hh + dc)]) for dc in range(4)],
                           self.xkmx[:, hh:hh + 1], ("xkmx", hh), False)
            for t in range(8):
                self.ws.push(self.xa_wkv, 0, KC, D + t * 256, 256)
            for t in range(8):
                wt, wk = self.ws.pop()
                for mt in range(2):
                    pbk, bk = self.bank()
                    for kc in range(KC):
                        cx.op(cx.pe, lambda kc=kc: nc.tensor.matmul(pbk[:, 0:256], lhsT=memT[:, kc, mt * P:(mt + 1) * P], rhs=wt[:, kc, :],
                                                                    start=(kc == 0), stop=(kc == KC - 1)),
                              reads=[wk, memkeys[kc]], writes=[bk])
                    cx.op(cx.act, lambda: nc.scalar.copy(out=self.Vm[:, mt, t * 256:(t + 1) * 256], in_=pbk[:, 0:256]),
                          reads=[bk], writes=[("Vm", mt, t)])
                self.ws.release()
            cx.barrier()

    def ln_feat(self, idx):
        cx, nc = self.cx, self.nc
        xT32, xTb = self.xT32, self.xTb
        p1, b1 = self.bank()
        p2, b2 = self.bank()
        for oc in range(KC):
            sq, sk = self.f32r.next()
            cx.op(cx.act, lambda sq=sq: nc.scalar.activation(out=sq[:], in_=xT32[:, oc, :], func=AF.Square),
                  reads=[("xT", "32", oc)], writes=[sk])
            cx.op(cx.pe, lambda: nc.tensor.matmul(p1[:], lhsT=self.ones32[:], rhs=xT32[:, oc, :], start=(oc == 0), stop=(oc == KC - 1)),
                  reads=[("xT", "32", oc), "ones32"], writes=[b1])
            cx.op(cx.pe, lambda sq=sq: nc.tensor.matmul(p2[:], lhsT=self.ones32[:], rhs=sq[:], start=(oc == 0), stop=(oc == KC - 1)),
                  reads=[sk, "ones32"], writes=[b2])
        mean, rstd = self.ln_stats(p1, b1, p2, b2)
        for oc in range(KC):
            t, tk = self.f32r.next()
            cx.op(cx.dve, lambda t=t: nc.vector.tensor_tensor(out=t[:], in0=xT32[:, oc, :], in1=mean[:], op=ALU.subtract),
                  reads=[("xT", "32", oc), "ln_mean"], writes=[tk])
            cx.op(cx.dve, lambda t=t: nc.vector.tensor_tensor(out=t[:], in0=t[:], in1=rstd[:], op=ALU.mult),
                  reads=[tk, "ln_rstd"], writes=[tk])
            g = self.vecs[:, V_LNG + idx * 16 + oc:V_LNG + idx * 16 + oc + 1]
            b = self.vecs[:, V_LNB + idx * 16 + oc:V_LNB + idx * 16 + oc + 1]
            cx.op(cx.act, lambda t=t: nc.scalar.activation(out=xT32[:, oc, :], in_=t[:], func=AF.Identity, bias=b, scale=g),
                  reads=[tk, "vecs"], writes=[("xT", "32", oc)])
            cx.op(cx.act, lambda t=t: nc.scalar.activation(out=xTb[:, oc, HALO:HALO + N], in_=t[:], func=AF.Identity, bias=b, scale=g),
                  reads=[tk, "vecs"], writes=[("xT", "b", oc)])

    def ln_stats(self, p1, b1, p2, b2):
        cx, nc = self.cx, self.nc
        mean, rstd = self.ln_mean, self.ln_rstd
        cx.op(cx.act, lambda: nc.scalar.copy(out=mean[:], in_=p1[:]), reads=[b1], writes=["ln_mean"])
        cx.op(cx.dve, lambda: nc.vector.tensor_tensor(out=rstd[:], in0=mean[:], in1=mean[:], op=ALU.mult),
              reads=["ln_mean"], writes=["ln_rstd"])
        cx.op(cx.dve, lambda: nc.vector.tensor_tensor(out=rstd[:], in0=p2[:], in1=rstd[:], op=ALU.subtract),
              reads=[b2, "ln_rstd"], writes=["ln_rstd"])
        cx.op(cx.dve, lambda: nc.vector.tensor_scalar(out=rstd[:], in0=rstd[:], scalar1=0.0, scalar2=LN_EPS, op0=ALU.max, op1=ALU.add),
              reads=["ln_rstd"], writes=["ln_rstd"])
        cx.op(cx.act, lambda: nc.scalar.activation(out=rstd[:], in_=rstd[:], func=AF.Sqrt), reads=["ln_rstd"], writes=["ln_rstd"])
        cx.op(cx.dve, lambda: nc.vector.reciprocal(out=rstd[:], in_=rstd[:]), reads=["ln_rstd"], writes=["ln_rstd"])
        return mean, rstd

    def resid_evac(self, oc, pbk, bk):
        cx, nc = self.cx, self.nc
        cx.op(cx.dve, lambda: nc.vector.scalar_tensor_tensor(out=self.xT32[:, oc, :], in0=self.xT32[:, oc, :], scalar=ALPHA, in1=pbk[:],
                                                             op0=ALU.mult, op1=ALU.add),
              reads=[bk, ("xT", "32", oc)], writes=[("xT", "32", oc)])

    def store_tokmajor(self, dst_ap):
        cx, nc = self.cx, self.nc
        otok = self.xtok
        for tt in range(4):
            for k4 in range(4):
                pbk, bk = self.bank()
                for j in range(4):
                    kc = k4 * 4 + j
                    cx.op(cx.pe, lambda j=j, kc=kc: nc.tensor.transpose(out=pbk[:, j * P:(j + 1) * P], in_=self.xT32[:, kc, tt * P:(tt + 1) * P],
                                                                        identity=self.ident32),
                          reads=[("xT", "32", kc), "cm32"], writes=[bk])
                cx.op(cx.act, lambda: nc.scalar.copy(out=otok[:, tt, k4 * 512:(k4 + 1) * 512], in_=pbk[:]), reads=[bk], writes=["xtok"])
        cx.dma(cx.sp, self.sem_st, dst_ap.rearrange("t p d -> p t d"), otok[:], reads=["xtok"], writes=[("dramout", id(dst_ap))])

    def group(self, m):
        cx, nc = self.cx, self.nc
        sb = self.sb
        xT32, xTb = self.xT32, self.xTb
        xkeys_b = [("xT", "b", kc) for kc in range(KC)]
        with ExitStack() as st:
            self.f32r = Rot(cx, st, "f32G", [P, N], F32, 4)
            self.ln_mean = sb(st, "ln_mean", [P, N], F32)
            self.ln_rstd = sb(st, "ln_rstd", [P, N], F32)
            self.sqr = Rot(cx, st, "sqG", [P, N], BF16, 2)
            self.colr = Rot(cx, st, "colG", [P, 1], F32, 2)
            with ExitStack() as st1:
                self.xtok = sb(st1, "xtokG", [P, 4, D], F32)
                self.load_transpose(st1, self.xo[m], 4, xT32, xTb, HALO, "xT")
                if self.layer == 0:
                    self.load_halo(st1, m)
                cx.barrier()
            if self.layer == 0:
                self.mixer_even(m)
            else:
                self.mixer_odd(m)
            self.ln_feat(0)
            if self.dbg:
                with ExitStack() as std:
                    self.xtok = sb(std, "xtokD1", [P, 4, D], F32)
                    self.store_tokmajor(self.dbg1[m])
                    cx.barrier()
            self.cross_attn()
            self.ln_feat(1)
            if self.dbg:
                with ExitStack() as std:
                    self.xtok = sb(std, "xtokD2", [P, 4, D], F32)
                    self.store_tokmajor(self.dbg2[m])
                    cx.barrier()
            self.ffn()
            self.ln_feat(2)
            with ExitStack() as st5:
                self.xtok = sb(st5, "xtokO", [P, 4, D], F32)
                self.store_tokmajor(self.out[m])
                cx.barrier()

    def load_halo(self, st, m):
        cx, nc = self.cx, self.nc
        xh = self.sb(st, "xhalo", [HALO, D], F32)
        cx.dma(cx.sp, self.sem_halo, xh[:], self.xh[m], writes=["xhalo"])
        for k4 in range(4):
            pbk, bk = self.bank()
            for j in range(4):
                kc = k4 * 4 + j
                cx.op(cx.pe, lambda j=j, kc=kc: nc.tensor.transpose(out=pbk[:, j * HALO:(j + 1) * HALO], in_=xh[:, kc * P:(kc + 1) * P],
                                                                    identity=self.cm32[0:HALO, 0:HALO]),
                      reads=["xhalo", "cm32"], writes=[bk])
            cx.op(cx.dve, lambda: nc.vector.tensor_copy(out=self.xTb[:, k4 * 4:(k4 + 1) * 4, 0:HALO],
                                                        in_=pbk[:, 0:4 * HALO].rearrange("p (j t) -> p j t", j=4)),
                  reads=[bk], writes=[("xT", "h", k4)])

    def mixer_even(self, m):
        cx, nc = self.cx, self.nc
        sb = self.sb
        xT32, xTb = self.xT32, self.xTb
        lam_init = 0.8 - 0.6 * math.exp(-0.3 * 0)
        with ExitStack() as st:
            qT = sb(st, "qT", [P, 8, N], BF16)
            cTb = sb(st, "cTb", [P, 8, N], BF16)
            aTb = sb(st, "aTb", [P, 8, N], BF16)
            Ct = sb(st, "CtG", [P, N], F32)
            St = sb(st, "StG", [P, N], F32)
            qmx = sb(st, "qmx", [P, 8], F32)
            negB = sb(st, "negB", [P, 8], F32)
            lamt = sb(st, "lamt", [P, 4], F32)
            self.rawr = Rot(cx, st, "rawG", [P, N], BF16, 2)
            xmain = lambda kc: (xTb[:, kc, HALO:HALO + N], [("xT", "b", kc)])
            with ExitStack() as st2:
                tabkeys = self.rope_tables(st2, self.poso[:, m * N:(m + 1) * N], N, V_INVF64, Ct[:], St[:], "G")
                self.lambda_scalar(st2, lamt, lam_init)
                cx.barrier()

            def evac_q(oc, pbk, bk):
                self.rotary_evac(pbk, bk, N, self.R64b, Ct[:], St[:], tabkeys, qT[:, oc, :], [("qT", oc)])
                self.sqmax([(qT[:, oc, :], [("qT", oc)])], qmx[:, oc:oc + 1], ("qmx", oc), False)

            self.lin_feat(self.w_in, 0, 1024, xmain, KC, evac_q)
            cx.op(cx.dve, lambda: nc.vector.tensor_tensor(out=negB[:], in0=qmx[:], in1=self.kmx[:], op=ALU.mult),
                  reads=[("qmx", oc) for oc in range(8)] + ["kmx"], writes=["negB"])
            cx.op(cx.act, lambda: nc.scalar.activation(out=negB[:], in_=negB[:], func=AF.Sqrt), reads=["negB"], writes=["negB"])
            cx.op(cx.dve, lambda: nc.vector.tensor_scalar(out=negB[:], in0=negB[:], scalar1=-0.125, scalar2=None, op0=ALU.mult),
                  reads=["negB"], writes=["negB"])

            stc = ExitStack()
            uT = sb(stc, "uT", [P, 8, HALO + N], F32)
            acc = sb(stc, "cacc", [P, 8, N], F32)
            for t in range(4):
                self.ws.push(self.w_in, 0, KC, 3072 + t * 256, 256)
                self.ws.push(self.w_in, 0, KC, 4096 + t * 256, 256)
            for t in range(4):
                wv_, wvk = self.ws.pop()
                wg_, wgk = self.ws.pop()
                for j in range(2):
                    oc = t * 2 + j
                    pv, bv = self.bank()
                    pg, bg = self.bank()
                    ph, bh = self.bank()
                    for (pp, bb, ww, wwk) in ((pv, bv, wv_, wvk), (pg, bg, wg_, wgk)):
                        for kc in range(KC):
                            cx.op(cx.pe, lambda kc=kc, pp=pp, ww=ww: nc.tensor.matmul(pp[:], lhsT=ww[:, kc, j * P:(j + 1) * P], rhs=xTb[:, kc, HALO:HALO + N],
                                                                                      start=(kc == 0), stop=(kc == KC - 1)),
                                  reads=[wwk, ("xT", "b", kc)], writes=[bb])
                    for hi, (ww, wwk) in enumerate(((wv_, wvk), (wg_, wgk))):
                        for kc in range(KC):
                            cx.op(cx.pe, lambda kc=kc, ww=ww, hi=hi: nc.tensor.matmul(ph[:, hi * HALO:(hi + 1) * HALO], lhsT=ww[:, kc, j * P:(j + 1) * P],
                                                                                      rhs=xTb[:, kc, 0:HALO], start=(kc == 0), stop=(kc == KC - 1)),
                                  reads=[wwk, ("xT", "h", kc // 4)], writes=[bh])
                    sg, sgk = self.f32r.next()
                    cx.op(cx.act, lambda sg=sg: nc.scalar.activation(out=sg[:], in_=pg[:], func=AF.Sigmoid), reads=[bg], writes=[sgk])
                    cx.op(cx.dve, lambda sg=sg: nc.vector.tensor_tensor(out=uT[:, oc, HALO:HALO + N], in0=pv[:], in1=sg[:], op=ALU.mult),
                          reads=[bv, sgk], writes=[("uT", oc)])
                    sh, shk = self.f32r.next()
                    cx.op(cx.act, lambda sh=sh: nc.scalar.activation(out=sh[:, 0:HALO], in_=ph[:, HALO:2 * HALO], func=AF.Sigmoid), reads=[bh], writes=[shk])
                    cx.op(cx.dve, lambda sh=sh: nc.vector.tensor_tensor(out=uT[:, oc, 0:HALO], in0=ph[:, 0:HALO], in1=sh[:, 0:HALO], op=ALU.mult),
                          reads=[bh, shk], writes=[("uTh", oc)])
                self.ws.release()
                self.ws.release()

            for k in range(31):
                for oc in range(8):
                    cw = self.vecs[:, V_CW + oc * 31 + k:V_CW + oc * 31 + k + 1]
                    src = uT[:, oc, 2 + k:2 + k + N]
                    if k == 0:
                        cb = self.vecs[:, V_CB + oc:V_CB + oc + 1]
                        cx.op(cx.dve, lambda src=src, cw=cw, cb=cb: nc.vector.tensor_scalar(out=acc[:, oc, :], in0=src, scalar1=cw, scalar2=cb,
                                                                                           op0=ALU.mult, op1=ALU.add),
                              reads=[("uT", oc), ("uTh", oc), "vecs"], writes=[("acc", oc)])
                    else:
                        cx.op(cx.dve, lambda src=src, cw=cw: nc.vector.scalar_tensor_tensor(out=acc[:, oc, :], in0=src, scalar=cw, in1=acc[:, oc, :],
                                                                                           op0=ALU.mult, op1=ALU.add),
                              reads=[("uT", oc), ("uTh", oc), "vecs", ("acc", oc)], writes=[("acc", oc)])
            p1, b1 = self.bank()
            p2, b2 = self.bank()
            for oc in range(8):
                sq, sk = self.f32r.next()
                cx.op(cx.act, lambda sq=sq: nc.scalar.activation(out=sq[:], in_=acc[:, oc, :], func=AF.Square), reads=[("acc", oc)], writes=[sk])
                cx.op(cx.pe, lambda: nc.tensor.matmul(p1[:], lhsT=self.onesc32[:], rhs=acc[:, oc, :], start=(oc == 0), stop=(oc == 7)),
                      reads=[("acc", oc), "onesc32"], writes=[b1])
                cx.op(cx.pe, lambda sq=sq: nc.tensor.matmul(p2[:], lhsT=self.onesc32[:], rhs=sq[:], start=(oc == 0), stop=(oc == 7)),
                      reads=[sk, "onesc32"], writes=[b2])
            mean, rstd = self.ln_stats(p1, b1, p2, b2)
            for oc in range(8):
                t, tk = self.f32r.next()
                cx.op(cx.dve, lambda t=t: nc.vector.tensor_tensor(out=t[:], in0=acc[:, oc, :], in1=mean[:], op=ALU.subtract),
                      reads=[("acc", oc), "ln_mean"], writes=[tk])
                cx.op(cx.dve, lambda t=t: nc.vector.tensor_tensor(out=t[:], in0=t[:], in1=rstd[:], op=ALU.mult), reads=[tk, "ln_rstd"], writes=[tk])
                g = self.vecs[:, V_CLG + oc:V_CLG + oc + 1]
                b = self.vecs[:, V_CLB + oc:V_CLB + oc + 1]
                cx.op(cx.act, lambda t=t: nc.scalar.activation(out=cTb[:, oc, :], in_=t[:], func=AF.Silu, bias=b, scale=g),
                      reads=[tk, "vecs"], writes=[("cTb", oc)])

            cx.barrier()
            stc.close()
            self.diff_attention(st, m, qT, negB, lamt, aTb, lam_init)

            def rhs_o(kc):
                if kc < 8:
                    return aTb[:, kc, :], [("aTb", kc)]
                return cTb[:, kc - 8, :], [("cTb", kc - 8)]

            self.lin_feat(self.w_out, 0, D, rhs_o, KC, self.resid_evac)
            cx.barrier()

    def lambda_scalar(self, st, lamt, lam_init):
        cx, nc = self.cx, self.nc
        lp = self.sb(st, "lp", [1, 256], F32)
        pr = self.sb(st, "lpp", [1, 128], F32)
        s2 = self.sb(st, "lps", [1, 4], F32)
        cx.dma(cx.sp, self.sem_lp, lp[:], self.lam_in, writes=["lp"])
        lp4 = lp[:].rearrange("p (a d) -> p a d", a=4)
        pr2 = pr[:].rearrange("p (a d) -> p a d", a=2)
        cx.op(cx.dve, lambda: nc.vector.tensor_tensor(out=pr2[:, 0, :], in0=lp4[:, 0, :], in1=lp4[:, 1, :], op=ALU.mult), reads=["lp"], writes=["lpp0"])
        cx.op(cx.dve, lambda: nc.vector.tensor_tensor(out=pr2[:, 1, :], in0=lp4[:, 2, :], in1=lp4[:, 3, :], op=ALU.mult), reads=["lp"], writes=["lpp1"])
        cx.op(cx.dve, lambda: nc.vector.reduce_sum(out=s2[:, 0:2], in_=pr2, axis=AX.X), reads=["lpp0", "lpp1"], writes=["lps"])
        cx.op(cx.act, lambda: nc.scalar.activation(out=s2[:, 2:4], in_=s2[:, 0:2], func=AF.Exp), reads=["lps"], writes=["lps2"])
        cx.op(cx.dve, lambda: nc.vector.tensor_tensor(out=s2[:, 0:1], in0=s2[:, 3:4], in1=s2[:, 2:3], op=ALU.subtract), reads=["lps2"], writes=["lps3"])
        cx.op(cx.dve, lambda: nc.vector.tensor_scalar(out=s2[:, 0:1], in0=s2[:, 0:1], scalar1=-lam_init, scalar2=None, op0=ALU.add),
              reads=["lps3"], writes=["lps3"])
        pbk, bk = self.bank()
        cx.op(cx.pe, lambda: nc.tensor.matmul(pbk[:, 0:1], lhsT=self.ones32[0:1, :], rhs=s2[:, 0:1], start=True, stop=True),
              reads=["lps3", "ones32"], writes=[bk])
        cx.op(cx.dve, lambda: nc.vector.tensor_scalar(out=lamt[:, 0:1], in0=pbk[:, 0:1], scalar1=float(D), scalar2=None, op0=ALU.mult),
              reads=[bk], writes=["lamt"])
        cx.op(cx.dve, lambda: nc.vector.tensor_scalar(out=lamt[:, 1:2], in0=self.vecs[:, V_SUBG:V_SUBG + 1], scalar1=1.0 - lam_init, scalar2=None,
                                                      op0=ALU.mult), reads=["vecs"], writes=["lamt_g"])

    def diff_attention(self, st, m, qT, negB, lamt, aTb, lam_init):
        cx, nc = self.cx, self.nc
        sb = self.sb
        kseg = [sb(st, f"kseg{i}", [P, 2048], BF16) for i in range(2)]
        vseg = [sb(st, f"vseg{i}", [P, 16, P], BF16) for i in range(2)]
        pr = Rot(cx, st, "pexp", [P, N], BF16, 6)
        self.nrot = 4
        self.bank_i = 0
        acc_b = [(self.pb[4 + i], ("pb", 4 + i)) for i in range(4)]
        nseg = m + 1
        loads = [(h, sg) for h in range(8) for sg in range(nseg)]

        def issue(i):
            h, sg = loads[i]
            s = i % 2
            cx.dmas(cx.sp, self.sem_kv[s],
                    [(kseg[s][:], self.KT[h, :, sg * 2048:(sg + 1) * 2048]),
                     (vseg[s][:], self.Vs[sg * 2048:(sg + 1) * 2048, h * P:(h + 1) * P].rearrange("(b p) d -> p b d", p=P))],
                    writes=[("kseg", s), ("vseg", s)])

        issue(0)
        for i, (h, sg) in enumerate(loads):
            if i + 1 < len(loads):
                issue(i + 1)
            s = i % 2
            for kb in range(16):
                first = (sg == 0 and kb == 0)
                last = (sg == nseg - 1 and kb == 15)
                ps = []
                for c in range(2):
                    pbk, bk = self.bank()
                    cx.op(cx.pe, lambda c=c, pbk=pbk: nc.tensor.matmul(pbk[:], lhsT=kseg[s][c * 64:(c + 1) * 64, kb * P:(kb + 1) * P],
                                                                       rhs=qT[c * 64:(c + 1) * 64, h, :], start=True, stop=True),
                          reads=[("kseg", s), ("qT", h)], writes=[bk])
                    pt, pk = pr.next()
                    cx.op(cx.act, lambda pt=pt, pbk=pbk: nc.scalar.activation(out=pt[:], in_=pbk[:], func=AF.Exp, bias=negB[:, h:h + 1], scale=0.125),
                          reads=[bk, "negB"], writes=[pk])
                    if sg == nseg - 1:
                        cx.op(cx.dve, lambda pt=pt: nc.vector.tensor_tensor(out=pt[:], in0=pt[:], in1=self.cmk[:, kb, :], op=ALU.mult),
                              reads=[pk, "cmk"], writes=[pk])
                    ps.append((pt, pk))
                for c in range(2):
                    pt, pk = ps[c]
                    (po, bo), (pd, bd) = acc_b[2 * c], acc_b[2 * c + 1]
                    cx.op(cx.pe, lambda pt=pt, po=po: nc.tensor.matmul(po[:], lhsT=vseg[s][:, kb, :], rhs=pt[:], start=first, stop=last),
                          reads=[("vseg", s), pk], writes=[bo])
                    cx.op(cx.pe, lambda pt=pt, pd=pd: nc.tensor.matmul(pd[:], lhsT=self.ones_b[:], rhs=pt[:], start=first, stop=last),
                          reads=["ones_b", pk], writes=[bd])
            if sg == nseg - 1:
                r1, r1k = self.f32r.next()
                r2, r2k = self.f32r.next()
                (po1, bo1), (pd1, bd1), (po2, bo2), (pd2, bd2) = acc_b
                cx.op(cx.dve, lambda r1=r1: nc.vector.reciprocal(out=r1[:], in_=pd1[:]), reads=[bd1], writes=[r1k])
                cx.op(cx.dve, lambda r2=r2: nc.vector.reciprocal(out=r2[:], in_=pd2[:]), reads=[bd2], writes=[r2k])
                cx.op(cx.dve, lambda r1=r1: nc.vector.tensor_tensor(out=r1[:], in0=po1[:], in1=r1[:], op=ALU.mult), reads=[bo1, r1k], writes=[r1k])
                cx.op(cx.dve, lambda r2=r2: nc.vector.tensor_tensor(out=r2[:], in0=po2[:], in1=r2[:], op=ALU.mult), reads=[bo2, r2k], writes=[r2k])
                cx.op(cx.dve, lambda r1=r1, r2=r2: nc.vector.scalar_tensor_tensor(out=r1[:], in0=r2[:], scalar=lamt[:, 0:1], in1=r1[:],
                                                                                 op0=ALU.mult, op1=ALU.add),
                      reads=[r1k, r2k, "lamt"], writes=[r1k])
                sq, sqk = self.f32r.next()
                cx.op(cx.act, lambda sq=sq, r1=r1: nc.scalar.activation(out=sq[:], in_=r1[:], func=AF.Square), reads=[r1k], writes=[sqk])
                pbk, bk = self.bank()
                cx.op(cx.pe, lambda sq=sq, pbk=pbk: nc.tensor.matmul(pbk[:], lhsT=self.onesh32[:], rhs=sq[:], start=True, stop=True),
                      reads=[sqk, "onesh32"], writes=[bk])
                cx.op(cx.act, lambda sq=sq, pbk=pbk: nc.scalar.activation(out=sq[:], in_=pbk[:], func=AF.Sqrt, bias=LN_EPS),
                      reads=[bk], writes=[sqk])
                cx.op(cx.dve, lambda sq=sq: nc.vector.reciprocal(out=sq[:], in_=sq[:]), reads=[sqk], writes=[sqk])
                cx.op(cx.dve, lambda sq=sq, r1=r1: nc.vector.tensor_tensor(out=r1[:], in0=r1[:], in1=sq[:], op=ALU.mult), reads=[r1k, sqk], writes=[r1k])
                cx.op(cx.dve, lambda r1=r1: nc.vector.tensor_scalar(out=aTb[:, h, :], in0=r1[:], scalar1=lamt[:, 1:2], scalar2=None, op0=ALU.mult),
                      reads=[r1k, "lamt_g"], writes=[("aTb", h)])
        self.nrot = 8
        self.bank_i = 0

    def cross_attn(self):
        cx, nc = self.cx, self.nc
        sb = self.sb
        xT32, xTb = self.xT32, self.xTb
        scale = 512 ** -0.5
        with ExitStack() as st:
            qx = sb(st, "qx", [P, KC, N], BF16)
            oTb = sb(st, "oTb", [P, KC, N], BF16)
            qxm = sb(st, "qxm", [P, 4], F32)
            negBx = sb(st, "negBx", [P, 4], F32)
            pr = Rot(cx, st, "pexpx", [P, N], BF16, 4)
            xmain = lambda kc: (xTb[:, kc, HALO:HALO + N], [("xT", "b", kc)])

            def evac_q(oc, pbk, bk):
                cx.op(cx.act, lambda: nc.scalar.copy(out=qx[:, oc, :], in_=pbk[:]), reads=[bk], writes=[("qx", oc)])

            self.lin_feat(self.xa_wq, 0, D, xmain, KC, evac_q)
            for hh in range(4):
                self.sqmax([(qx[:, 4 * hh + dc, :], [("qx", 4 * hh + dc)]) for dc in range(4)], qxm[:, hh:hh + 1], ("qxm", hh), False)
            cx.op(cx.dve, lambda: nc.vector.tensor_tensor(out=negBx[:], in0=qxm[:], in1=self.xkmx[:], op=ALU.mult),
                  reads=[("qxm", hh) for hh in range(4)] + [("xkmx", hh) for hh in range(4)], writes=["negBx"])
            cx.op(cx.act, lambda: nc.scalar.activation(out=negBx[:], in_=negBx[:], func=AF.Sqrt), reads=["negBx"], writes=["negBx"])
            cx.op(cx.dve, lambda: nc.vector.tensor_scalar(out=negBx[:], in0=negBx[:], scalar1=-scale, scalar2=None, op0=ALU.mult),
                  reads=["negBx"], writes=["negBx"])
            for hh in range(4):
                pts = []
                for mt in range(2):
                    pbk, bk = self.bank()
                    for dc in range(4):
                        cx.op(cx.pe, lambda dc=dc, pbk=pbk: nc.tensor.matmul(pbk[:], lhsT=self.KmT[:, 4 * hh + dc, mt * P:(mt + 1) * P], rhs=qx[:, 4 * hh + dc, :],
                                                                             start=(dc == 0), stop=(dc == 3)),
                              reads=[("KmT", 4 * hh + dc), ("qx", 4 * hh + dc)], writes=[bk])
                    pt, pk = pr.next()
                    cx.op(cx.act, lambda pt=pt, pbk=pbk: nc.scalar.activation(out=pt[:], in_=pbk[:], func=AF.Exp, bias=negBx[:, hh:hh + 1], scale=scale),
                          reads=[bk, "negBx"], writes=[pk])
                    pts.append((pt, pk))
                pd, bd = self.bank()
                for mt in range(2):
                    cx.op(cx.pe, lambda mt=mt: nc.tensor.matmul(pd[:], lhsT=self.ones_b[:], rhs=pts[mt][0][:], start=(mt == 0), stop=(mt == 1)),
                          reads=["ones_b", pts[mt][1]], writes=[bd])
                rd, rdk = self.f32r.next()
                cx.op(cx.dve, lambda rd=rd: nc.vector.reciprocal(out=rd[:], in_=pd[:]), reads=[bd], writes=[rdk])
                for dvc in range(4):
                    po, bo = self.bank()
                    for mt in range(2):
                        cx.op(cx.pe, lambda mt=mt, po=po: nc.tensor.matmul(po[:], lhsT=self.Vm[:, mt, (4 * hh + dvc) * P:(4 * hh + dvc + 1) * P], rhs=pts[mt][0][:],
                                                                           start=(mt == 0), stop=(mt == 1)),
                              reads=[("Vm", mt, (4 * hh + dvc) // 2), pts[mt][1]], writes=[bo])
                    cx.op(cx.dve, lambda po=po, rd=rd, dvc=dvc: nc.vector.tensor_tensor(out=oTb[:, 4 * hh + dvc, :], in0=po[:], in1=rd[:], op=ALU.mult),
                          reads=[bo, rdk], writes=[("oTb", 4 * hh + dvc)])
            self.lin_feat(self.xa_wo, 0, D, lambda kc: (oTb[:, kc, :], [("oTb", kc)]), KC, self.resid_evac)
            cx.barrier()

    def ffn(self):
        cx, nc = self.cx, self.nc
        sb = self.sb
        xTb = self.xTb
        with ExitStack() as st:
            hT = sb(st, "hT", [P, FC, N], BF16)
            for t in range(FC // 2):
                self.ws.push(self.ffn_w_in, 0, KC, t * 256, 256)
                self.ws.push(self.ffn_w_in, 0, KC, D_FF + t * 256, 256)
            for t in range(FC // 2):
                wg_, wgk = self.ws.pop()
                wu_, wuk = self.ws.pop()
                for j in range(2):
                    fc = t * 2 + j
                    pg, bg = self.bank()
                    pu, bu = self.bank()
                    for (pp, bb, ww, wwk) in ((pg, bg, wg_, wgk), (pu, bu, wu_, wuk)):
                        for kc in range(KC):
                            cx.op(cx.pe, lambda kc=kc, pp=pp, ww=ww: nc.tensor.matmul(pp[:], lhsT=ww[:, kc, j * P:(j + 1) * P], rhs=xTb[:, kc, HALO:HALO + N],
                                                                                      start=(kc == 0), stop=(kc == KC - 1)),
                                  reads=[wwk, ("xT", "b", kc)], writes=[bb])
                    sg, sgk = self.f32r.next()
                    cx.op(cx.act, lambda sg=sg: nc.scalar.activation(out=sg[:], in_=pg[:], func=AF.Silu), reads=[bg], writes=[sgk])
                    cx.op(cx.dve, lambda sg=sg, fc=fc: nc.vector.tensor_tensor(out=hT[:, fc, :], in0=pu[:], in1=sg[:], op=ALU.mult),
                          reads=[bu, sgk], writes=[("hT", fc)])
                self.ws.release()
                self.ws.release()
            for oc in range(KC):
                for hf in range(2):
                    self.ws.push(self.ffn_w_out, hf * 22 * P, 22, oc * P, P)
            for oc in range(KC):
                pbk, bk = self.bank()
                for hf in range(2):
                    wt, wk = self.ws.pop()
                    for kc in range(22):
                        fc = hf * 22 + kc
                        cx.op(cx.pe, lambda kc=kc, fc=fc, wt=wt: nc.tensor.matmul(pbk[:], lhsT=wt[:, kc, :], rhs=hT[:, fc, :],
                                                                                  start=(fc == 0), stop=(fc == FC - 1)),
                              reads=[wk, ("hT", fc)], writes=[bk])
                    self.ws.release()
                self.resid_evac(oc, pbk, bk)
            cx.barrier()

    def phaseA_odd(self):
        cx, nc = self.cx, self.nc
        S = self.S
        sb = self.sb
        self.KIT = self.dscr("KIT", [P, S], BF16)
        self.wwi = sb(self.st, "wwi", [P, KC, 4], BF16)
        self.tb0 = sb(self.st, "tb0", [P, N], F32)
        self.cap = sb(self.st, "cap", [P, 4, 2048], BF16)
        with ExitStack() as st:
            self.cmk = sb(st, "cmk", [P, 16, N], BF16)
            cx.dma(cx.pool, self.sem_cmask, self.cmk[:], self.cmask.rearrange("k p n -> p k n"), writes=["cmk"])
            self.xtok = sb(st, "xtokA", [P, 4, D], F32)
            xTa = sb(st, "xTa", [P, KC, N], BF16)
            wk = sb(st, "wkA", [P, KC, 512], BF16)
            wv = sb(st, "wvA", [P, KC, 512], BF16)
            wki = sb(st, "wkiA", [P, KC, P], BF16)
            C8 = sb(st, "C8A", [P, N], F32)
            S8 = sb(st, "S8A", [P, N], F32)
            C6 = sb(st, "C6A", [P, N], F32)
            S6 = sb(st, "S6A", [P, N], F32)
            self.rawr = Rot(cx, st, "rawA", [P, N], BF16, 2)
            self.f32r = Rot(cx, st, "f32A", [P, N], F32, 4)
            self.sqr = Rot(cx, st, "sqA", [P, N], BF16, 2)
            self.colr = Rot(cx, st, "colA", [P, 1], F32, 2)
            kout = Rot(cx, st, "koutA", [P, N], BF16, 4, sems=self.sem_rot[0:4])
            vout = Rot(cx, st, "voutA", [P, N], BF16, 4, sems=self.sem_rot[4:8])
            sw = cx.dsem("wA")
            wsrc = lambda c0, n: self.w_in[:, c0:c0 + n].rearrange("(k p) n -> p k n", p=P)
            cx.dmas(cx.pool, sw, [(wk[:], wsrc(2048, 512)), (wv[:], wsrc(2560, 512)), (wki[:, :, 0:64], wsrc(3328, 64)),
                                  (wki[:, :, 64:128], wsrc(3328, 64)), (self.wwi[:], wsrc(3392, 4))],
                    writes=["wkA", "wvA", "wkiA", "wwi"])
            cx.dma(cx.sp, self.sem_lp, self.tb0[:], self.tb0_in, writes=["tb0"])
            cx.op(cx.dve, lambda: nc.vector.memset(self.kmx[:], 0.0), writes=["kmx"])
            for g in range(S // N):
                with ExitStack() as st2:
                    pos = self.posb[:, g * N:(g + 1) * N]
                    t8 = self.rope_tables(st2, pos, N, V_INVF128, C8[:], S8[:], "A8")
                    t6 = self.rope_tables(st2, pos, N, V_INVF64, C6[:], S6[:], "A6", self.sem_pos2)
                    self.load_transpose(st2, self.xb[g * N:(g + 1) * N, :].rearrange("(t p) d -> t p d", p=P), 4, None, xTa, 0, "xTa")
                    for oc in range(4):
                        pbk, bk = self.bank()
                        for kc in range(KC):
                            cx.op(cx.pe, lambda kc=kc: nc.tensor.matmul(pbk[:], lhsT=wk[:, kc, oc * P:(oc + 1) * P], rhs=xTa[:, kc, :],
                                                                        start=(kc == 0), stop=(kc == KC - 1)),
                                  reads=["wkA", ("xTa", "b", kc)], writes=[bk])
                        ko, kk, ksem = kout.next_s()
                        self.rotary_evac(pbk, bk, N, self.R128b, C8[:], S8[:], t8, ko[:], [kk])
                        self.sqmax([(ko[:], [kk])], self.kmx[:, oc:oc + 1], "kmx", True)
                        cx.dma(cx.sp, ksem, self.KT[oc, :, g * N:(g + 1) * N], ko[:], reads=[kk], writes=[("KT", oc, g)])
                    pbk, bk = self.bank()
                    for kc in range(KC):
                        cx.op(cx.pe, lambda kc=kc: nc.tensor.matmul(pbk[:], lhsT=wki[:, kc, :], rhs=xTa[:, kc, :], start=(kc == 0), stop=(kc == KC - 1)),
                              reads=["wkiA", ("xTa", "b", kc)], writes=[bk])
                    ko, kk, ksem = kout.next_s()
                    self.rotary_evac(pbk, bk, N, self.R64b, C6[:], S6[:], t6, ko[:], [kk])
                    cx.dma(cx.sp, ksem, self.KIT[:, g * N:(g + 1) * N], ko[:], reads=[kk], writes=[("KIT", g)])
                    for tt in range(4):
                        pbk, bk = self.bank()
                        for kc in range(KC):
                            cx.op(cx.pe, lambda kc=kc: nc.tensor.matmul(pbk[:], lhsT=xTa[:, kc, tt * P:(tt + 1) * P], rhs=wv[:, kc, :],
                                                                        start=(kc == 0), stop=(kc == KC - 1)),
                                  reads=["wvA", ("xTa", "b", kc)], writes=[bk])
                        vo, vk, vsem = vout.next_s()
                        cx.op(cx.act, lambda: nc.scalar.copy(out=vo[:], in_=pbk[:]), reads=[bk], writes=[vk])
                        cx.dma(cx.sp, vsem, self.Vs[g * N + tt * P:g * N + (tt + 1) * P, :], vo[:], reads=[vk], writes=[("Vs", g, tt)])
                    cx.barrier()
            for i in range(4):
                for k4 in range(4):
                    pbk, bk = self.bank()
                    pbb = pbk[:].bitcast(BF16)
                    for j in range(4):
                        kb = k4 * 4 + j
                        cx.op(cx.pe, lambda j=j, kb=kb: nc.tensor.transpose(out=pbb[:, j * P:(j + 1) * P], in_=self.cmk[:, kb, i * P:(i + 1) * P],
                                                                            identity=self.cmb[:, 0:P]),
                              reads=["cmk", "cmb"], writes=[bk])
                    cx.op(cx.dve, lambda: nc.vector.tensor_scalar(out=self.cap[:, i, k4 * 512:(k4 + 1) * 512], in0=pbb[:, 0:512],
                                                                  scalar1=3.0e38 + 3.0e30, scalar2=-3.0e30, op0=ALU.mult, op1=ALU.add),
                          reads=[bk], writes=[("cap", i, k4)])
            cx.barrier()

    def group_odd(self, m):
        cx, nc = self.cx, self.nc
        sb = self.sb
        with ExitStack() as st:
            self.f32r = Rot(cx, st, "f32G", [P, N], F32, 4)
            self.ln_mean = sb(st, "ln_mean", [P, N], F32)
            self.ln_rstd = sb(st, "ln_rstd", [P, N], F32)
            self.sqr = Rot(cx, st, "sqG", [P, N], BF16, 2)
            self.colr = Rot(cx, st, "colG", [P, 1], F32, 2)
            aTb = sb(st, "aTb", [P, KC, N], BF16)
            with ExitStack() as stm:
                qT = sb(stm, "qT", [P, KC, N], BF16)
                qiT = sb(stm, "qiT", [P, 2, N], BF16)
                wiS = sb(stm, "wiS", [P, 4, 4], F32)
                negB = sb(stm, "negB", [P, 4], F32)
                with ExitStack() as st1:
                    self.xTb = sb(st1, "xTb1", [P, KC, HALO + N], BF16)
                    with ExitStack() as st0:
                        self.xtok = sb(st0, "xtokG", [P, 4, D], F32)
                        self.load_transpose(st0, self.xo[m], 4, None, self.xTb, HALO, "xT")
                        cx.barrier()
                    self.ws = WStream(cx, st1, 4, 16 * 256, self.wsems)
                    self.odd_proj(st1, m, qT, qiT, wiS, negB)
                    cx.barrier()
                self.odd_attend(stm, m, qT, qiT, wiS, negB, aTb)
                cx.barrier()
            self.xT32 = sb(st, "xT32", [P, KC, N], F32)
            self.xTb = sb(st, "xTb", [P, KC, HALO + N], BF16)
            self.ws = WStream(cx, st, 4, 16 * 256, self.wsems)
            with ExitStack() as st0:
                self.xtok = sb(st0, "xtokG2", [P, 4, D], F32)
                self.load_transpose(st0, self.xo[m], 4, self.xT32, self.xTb, HALO, "xT")
                cx.barrier()
            self.lin_feat(self.w_out, 0, D, lambda kc: (aTb[:, kc, :], [("aTb", kc)]), KC, self.resid_evac)
            self.ln_feat(0)
            if self.dbg:
                with ExitStack() as std:
                    self.xtok = sb(std, "xtokD1", [P, 4, D], F32)
                    self.store_tokmajor(self.dbg1[m])
                    cx.barrier()
            self.cross_attn()
            self.ln_feat(1)
            if self.dbg:
                with ExitStack() as std:
                    self.xtok = sb(std, "xtokD2", [P, 4, D], F32)
                    self.store_tokmajor(self.dbg2[m])
                    cx.barrier()
            self.ffn()
            self.ln_feat(2)
            with ExitStack() as st5:
                self.xtok = sb(st5, "xtokO", [P, 4, D], F32)
                self.store_tokmajor(self.out[m])
                cx.barrier()

    def odd_proj(self, st, m, qT, qiT, wiS, negB):
        cx, nc = self.cx, self.nc
        sb = self.sb
        xTb = self.xTb
        C8 = sb(st, "C8G", [P, N], F32)
        S8 = sb(st, "S8G", [P, N], F32)
        C6 = sb(st, "C6G", [P, N], F32)
        S6 = sb(st, "S6G", [P, N], F32)
        qmx = sb(st, "qmx", [P, KC], F32)
        qmg = sb(st, "qmg", [P, 4], F32)
        self.rawr = Rot(cx, st, "rawG", [P, N], BF16, 2)
        pos = self.poso[:, m * N:(m + 1) * N]
        with ExitStack() as st2:
            t8 = self.rope_tables(st2, pos, N, V_INVF128, C8[:], S8[:], "G8")
            t6 = self.rope_tables(st2, pos, N, V_INVF64, C6[:], S6[:], "G6", self.sem_pos2)
            cx.barrier()
        xmain = lambda kc: (xTb[:, kc, HALO:HALO + N], [("xT", "b", kc)])

        def evac_q(oc, pbk, bk):
            self.rotary_evac(pbk, bk, N, self.R128b, C8[:], S8[:], t8, qT[:, oc, :], [("qT", oc)])
            self.sqmax([(qT[:, oc, :], [("qT", oc)])], qmx[:, oc:oc + 1], ("qmx", oc), False)

        self.lin_feat(self.w_in, 0, 2048, xmain, KC, evac_q)

        def evac_qi(oc, pbk, bk):
            self.rotary_evac(pbk, bk, N, self.R64b, C6[:], S6[:], t6, qiT[:, oc, :], [("qiT", oc)])

        self.lin_feat(self.w_in, 3072, 256, xmain, KC, evac_qi)
        for tt in range(4):
            pbk, bk = self.bank()
            for kc in range(KC):
                cx.op(cx.pe, lambda kc=kc: nc.tensor.matmul(pbk[:, 0:4], lhsT=xTb[:, kc, HALO + tt * P:HALO + (tt + 1) * P], rhs=self.wwi[:, kc, :],
                                                            start=(kc == 0), stop=(kc == KC - 1)),
                      reads=["wwi", ("xT", "b", kc)], writes=[bk])
            cx.op(cx.dve, lambda: nc.vector.tensor_scalar(out=wiS[:, tt, :], in0=pbk[:, 0:4], scalar1=1.0 / 16.0, scalar2=None, op0=ALU.mult),
                  reads=[bk], writes=["wiS"])
        cx.op(cx.dve, lambda: nc.vector.tensor_reduce(out=qmg[:], in_=qmx[:].rearrange("p (g r) -> p g r", r=4), axis=AX.X, op=ALU.max),
              reads=[("qmx", oc) for oc in range(KC)], writes=["qmg"])
        cx.op(cx.dve, lambda: nc.vector.tensor_tensor(out=negB[:], in0=qmg[:], in1=self.kmx[:, 0:4], op=ALU.mult), reads=["qmg", "kmx"], writes=["negB"])
        cx.op(cx.act, lambda: nc.scalar.activation(out=negB[:], in_=negB[:], func=AF.Sqrt), reads=["negB"], writes=["negB"])
        cx.op(cx.dve, lambda: nc.vector.tensor_scalar(out=negB[:], in0=negB[:], scalar1=-(128 ** -0.5), scalar2=None, op0=ALU.mult),
              reads=["negB"], writes=["negB"])

    def odd_attend(self, st, m, qT, qiT, wiS, negB, aTb):
        cx, nc = self.cx, self.nc
        sb = self.sb
        nseg = m + 1
        scale = 128 ** -0.5
        work = sb(st, "work", [P, 2048], F32)
        cand = sb(st, "cand", [P, 2048], F32)
        selb = sb(st, "selb", [P, 2048], BF16)
        maskT = sb(st, "maskT", [P, 16 * nseg, P], BF16)
        thr8 = sb(st, "thr8", [P, 8], F32)
        thr1 = sb(st, "thr1", [P, 1], F32)
        relr = Rot(cx, st, "relr", [P, N], F32, 2)
        kseg = [sb(st, f"kseg{i}", [P, 2048], BF16) for i in range(2)]
        vseg = [sb(st, f"vseg{i}", [P, 16, P], BF16) for i in range(2)]
        pr = Rot(cx, st, "pexp", [P, N], BF16, 4)
        wkeys = [("work", k) for k in range(4)]
        kvload = [0]
        self.kiT = sb(st, "kiT", [P, 2048 * nseg], BF16)
        cx.dma(cx.sp, self.sem_halo, self.kiT[:], self.KIT[:, 0:2048 * nseg], writes=["kiT"])

        def scores(i, sg):
            for kb5 in range(4):
                key0 = sg * 2048 + kb5 * 512
                wk_ = work[:, kb5 * 512:(kb5 + 1) * 512]
                cx.op(cx.dve, lambda: nc.vector.tensor_scalar(out=wk_, in0=self.tb0[:], scalar1=-TIE_EPS * key0, scalar2=None, op0=ALU.add),
                      reads=["tb0"], writes=[wkeys[kb5]])
                for h in range(4):
                    pbk, bk = self.bank()
                    po = (h % 2) * 64
                    cx.op(cx.pe, lambda: nc.tensor.matmul(pbk[:], lhsT=qiT[po:po + 64, h // 2, i * P:(i + 1) * P], rhs=self.kiT[po:po + 64, key0:key0 + 512],
                                                          start=True, stop=True),
                          reads=[("qiT", h // 2), "kiT"], writes=[bk])
                    rl, rk = relr.next()
                    cx.op(cx.act, lambda: nc.scalar.activation(out=rl[:], in_=pbk[:], func=AF.Relu), reads=[bk], writes=[rk])
                    cx.op(cx.dve, lambda: nc.vector.scalar_tensor_tensor(out=wk_, in0=rl[:], scalar=wiS[:, i, h:h + 1], in1=wk_, op0=ALU.mult, op1=ALU.add),
                          reads=[rk, "wiS", wkeys[kb5]], writes=[wkeys[kb5]])
            if sg == m:
                cx.op(cx.dve, lambda: nc.vector.tensor_tensor(out=work[:], in0=work[:], in1=self.cap[:, i, :], op=ALU.min),
                      reads=wkeys + ["cap"], writes=wkeys)

        for i in range(4):
            for sg in range(nseg):
                scores(i, sg)
                for r in range(32):
                    c8 = cand[:, sg * 256 + r * 8:sg * 256 + r * 8 + 8]
                    cx.op(cx.dve, lambda: nc.vector.max(out=c8, in_=work[:]), reads=wkeys, writes=["cand"])
                    if r < 31:
                        cx.op(cx.dve, lambda: nc.vector.match_replace(out=work[:], in_to_replace=c8, in_values=work[:], imm_value=-3.0e38),
                              reads=wkeys + ["cand"], writes=wkeys)
            L = nseg * 256
            for r in range(32):
                cx.op(cx.dve, lambda: nc.vector.max(out=thr8[:], in_=cand[:, 0:L]), reads=["cand"], writes=["thr8"])
                if r < 31:
                    cx.op(cx.dve, lambda: nc.vector.match_replace(out=cand[:, 0:L], in_to_replace=thr8[:], in_values=cand[:, 0:L], imm_value=-3.0e38),
                          reads=["cand", "thr8"], writes=["cand"])
            cx.op(cx.dve, lambda: nc.vector.tensor_scalar(out=thr1[:], in0=thr8[:, 7:8], scalar1=-2.0e30, scalar2=None, op0=ALU.max),
                  reads=["thr8"], writes=["thr1"])
            for sg in range(nseg):
                scores(i, sg)
                cx.op(cx.dve, lambda: nc.vector.tensor_scalar(out=selb[:], in0=work[:], scalar1=thr1[:, 0:1], scalar2=None, op0=ALU.is_ge),
                      reads=wkeys + ["thr1"], writes=["selb"])
                for k4 in range(4):
                    pbk, bk = self.bank()
                    pbb = pbk[:].bitcast(BF16)
                    for j in range(4):
                        kb = k4 * 4 + j
                        cx.op(cx.pe, lambda: nc.tensor.transpose(out=pbb[:, j * P:(j + 1) * P], in_=selb[:, kb * P:(kb + 1) * P], identity=self.cmb[:, 0:P]),
                              reads=["selb", "cmb"], writes=[bk])
                    cx.op(cx.act, lambda: nc.scalar.copy(out=maskT[:, sg * 16 + k4 * 4:sg * 16 + k4 * 4 + 4, :],
                                                         in_=pbb[:, 0:512].rearrange("p (j q) -> p j q", j=4)),
                          reads=[bk], writes=[("maskT", sg, k4)])
            self.nrot = 4
            self.bank_i = 0
            po_, bo_ = self.pb[4], ("pb", 4)
            pd_, bd_ = self.pb[5], ("pb", 5)
            for g in range(4):
                qg = qT[:, 4 * g:4 * g + 4, i * P:(i + 1) * P]
                qkeys = [("qT", 4 * g + r) for r in range(4)]
                for sg in range(nseg):
                    s = kvload[0] % 2
                    kvload[0] += 1
                    cx.dmas(cx.sp, self.sem_kv[s],
                            [(kseg[s][:], self.KT[g, :, sg * 2048:(sg + 1) * 2048]),
                             (vseg[s][:], self.Vs[sg * 2048:(sg + 1) * 2048, g * P:(g + 1) * P].rearrange("(b p) d -> p b d", p=P))],
                            writes=[("kseg", s), ("vseg", s)])
                    for kb in range(16):
                        first = (sg == 0 and kb == 0)
                        last = (sg == nseg - 1 and kb == 15)
                        pbk, bk = self.bank()
                        cx.op(cx.pe, lambda: nc.tensor.matmul(pbk[:], lhsT=kseg[s][:, kb * P:(kb + 1) * P], rhs=qg, start=True, stop=True),
                              reads=[("kseg", s)] + qkeys, writes=[bk])
                        pt, pk = pr.next()
                        cx.op(cx.act, lambda: nc.scalar.activation(out=pt[:], in_=pbk[:], func=AF.Exp, bias=negB[:, g:g + 1], scale=scale),
                              reads=[bk, "negB"], writes=[pk])
                        pt3 = pt[:].rearrange("p (h q) -> p h q", h=4)
                        mk = maskT[:, sg * 16 + kb:sg * 16 + kb + 1, :].to_broadcast([P, 4, P])
                        cx.op(cx.pool, lambda: nc.gpsimd.tensor_tensor(out=pt3, in0=pt3, in1=mk, op=ALU.mult),
                              reads=[pk, ("maskT", sg, kb // 4)], writes=[pk])
                        cx.op(cx.pe, lambda: nc.tensor.matmul(po_[:], lhsT=vseg[s][:, kb, :], rhs=pt[:], start=first, stop=last),
                              reads=[("vseg", s), pk], writes=[bo_])
                        cx.op(cx.pe, lambda: nc.tensor.matmul(pd_[:], lhsT=self.ones_b[:], rhs=pt[:], start=first, stop=last),
                              reads=["ones_b", pk], writes=[bd_])
                rd, rdk = self.f32r.next()
                cx.op(cx.dve, lambda: nc.vector.reciprocal(out=rd[:], in_=pd_[:]), reads=[bd_], writes=[rdk])
                cx.op(cx.dve, lambda: nc.vector.tensor_tensor(out=aTb[:, 4 * g:4 * g + 4, i * P:(i + 1) * P],
                                                              in0=po_[:].rearrange("p (h q) -> p h q", h=4),
                                                              in1=rd[:].rearrange("p (h q) -> p h q", h=4), op=ALU.mult),
                      reads=[bo_, rdk], writes=[("aTb", 4 * g + r) for r in range(4)])
            self.nrot = 8
            self.bank_i = 0

def _consts():
    cm = np.zeros((P, 3 * P), np.float32)
    cm[:, 0:P] = np.eye(P, dtype=np.float32)
    for mm in range(P):
        d = mm % 64
        if d < 8:
            cm[mm + 8, P + mm] = -1.0
        elif d < 16:
            cm[mm - 8, P + mm] = 1.0
        if mm < 16:
            cm[mm + 16, 2 * P + mm] = -1.0
        elif mm < 32:
            cm[mm - 16, 2 * P + mm] = 1.0
    invf64 = np.zeros(P, np.float32)
    invf128 = np.zeros(P, np.float32)
    f8 = (np.float32(1.0) / np.power(np.float32(ROPE_THETA), np.arange(8, dtype=np.float32) / np.float32(8))).astype(np.float32)
    f16 = (np.float32(1.0) / np.power(np.float32(ROPE_THETA), np.arange(16, dtype=np.float32) / np.float32(16))).astype(np.float32)
    for p in range(P):
        d = p % 64
        if d < 16:
            invf64[p] = f8[d % 8]
        if p < 32:
            invf128[p] = f16[p % 16]
    return cm, invf64, invf128


def _pm(v, n):
    return np.ascontiguousarray(np.asarray(v, np.float32).reshape(n, P).T)


def _vec(layer, ln_g, ln_b, conv_b=None, conv_ln_g=None, conv_ln_b=None, conv_w=None, subln_g=None):
    cm, invf64, invf128 = _consts()
    v = np.zeros((P, NVEC), np.float32)
    for i in range(3):
        v[:, V_LNG + 16 * i:V_LNG + 16 * (i + 1)] = _pm(ln_g[i], 16)
        v[:, V_LNB + 16 * i:V_LNB + 16 * (i + 1)] = _pm(ln_b[i], 16)
    if conv_b is not None:
        v[:, V_CB:V_CB + 8] = _pm(conv_b, 8)
        v[:, V_CLG:V_CLG + 8] = _pm(conv_ln_g, 8)
        v[:, V_CLB:V_CLB + 8] = _pm(conv_ln_b, 8)
        cw = np.asarray(conv_w, np.float32)
        v[:, V_CW:V_CW + 248] = cw.reshape(31, 8, P).transpose(2, 1, 0).reshape(P, 248)
        v[:, V_SUBG] = np.asarray(subln_g, np.float32)
    v[:, V_INVF64] = invf64
    v[:, V_INVF128] = invf128
    return v, cm


def _cmask(c):
    kb = np.arange(16)[:, None, None]
    p = np.arange(P)[None, :, None]
    r = np.arange(N)[None, None, :]
    return (128 * kb + p <= 512 * c + r).astype(np.float32)


STOP = [99, 9, 9]


def _get_nc(layer, NG, dbg=False):
    key = (layer, NG, dbg)
    if key not in _NC_CACHE:
        b = Builder(layer, NG, dbg)
        b.stop = STOP[0]
        b.stopA = STOP[1]
        _NC_CACHE[key] = (b.build(), b)
    return _NC_CACHE[key]


def _own(x_b, c, NG):
    S = x_b.shape[0]
    xc = x_b.reshape(S // N, 4, P, x_b.shape[1])
    return np.ascontiguousarray(xc[c::4][:NG])


def run_layer(layer, NG, x, mem, positions, w, dbg=False):
    nc, b = _get_nc(layer, NG, dbg)
    S = 2048 * NG
    in_maps = []
    for core in range(8):
        bi, c = core // 4, core % 4
        xb = np.ascontiguousarray(x[bi])
        pos = np.ascontiguousarray(positions[bi].astype(np.int32))
        mp = dict(w["common"])
        mp["xb"] = xb
        mp["posb"] = pos.reshape(1, S)
        mp["xo"] = _own(xb, c, NG)
        mp["poso"] = np.ascontiguousarray(pos.reshape(S // N, N)[c::4][:NG].reshape(1, NG * N))
        mp["cmask"] = _cmask(c)
        mp["memb"] = np.ascontiguousarray(mem[bi])
        if layer == 0:
            xh = np.zeros((NG, HALO, D), np.float32)
            for m in range(NG):
                ci = 4 * m + c
                if ci > 0:
                    xh[m] = xb[ci * N - HALO:ci * N]
            mp["xh"] = xh
        else:
            mp["tb0"] = np.ascontiguousarray(np.broadcast_to((-TIE_EPS * np.arange(N, dtype=np.float64)).astype(np.float32), (P, N)))
        in_maps.append(mp)
    res = run_bass_kernel_spmd(nc, in_maps, core_ids=list(range(8)))
    outs = {}
    for name in b.outs:
        full = np.zeros((2, S // N, 4, P, D), np.float32)
        for core in range(8):
            bi, c = core // 4, core % 4
            full[bi, c::4][:NG] = res.results[core][name]
        outs[name] = full.reshape(2, S, D)
    return outs


def layer_weights(layer, inp):
    j = layer // 2
    f = lambda a: np.ascontiguousarray(np.asarray(a, np.float32))
    if layer % 2 == 0:
        vec, cm = _vec(layer, inp["ln_g"][layer], inp["ln_b"][layer], inp["conv_b"][j], inp["conv_ln_g"][j], inp["conv_ln_b"][j],
                       inp["conv_w"][j], inp["diff_subln_g"][j])
        common = {"w_in": f(inp["w_in_even"][j]), "w_out": f(inp["w_out_even"][j]), "lam_in": f(inp["diff_lambda"][j]).reshape(1, 256)}
    else:
        vec, cm = _vec(layer, inp["ln_g"][layer], inp["ln_b"][layer])
        common = {"w_in": f(inp["w_in_odd"][j]), "w_out": f(inp["w_out_odd"][j])}
    common.update({"vec": vec, "cmat": cm, "xa_wq": f(inp["xa_wq"][layer]), "xa_wkv": f(inp["xa_wkv"][layer]), "xa_wo": f(inp["xa_wo"][layer]),
                   "ffn_w_in": f(inp["ffn_w_in"][layer]), "ffn_w_out": f(inp["ffn_w_out"][layer])})
    return {"common": common}


def kernel(**inputs):
    x = np.asarray(inputs["x"], np.float32)
    S = x.shape[1]
    NG = S // 2048
    mem = np.asarray(inputs["mem"], np.float32)
    pos = np.asarray(inputs["positions"])
    for layer in range(DEPTH):
        w = layer_weights(layer, inputs)
        x = run_layer(layer, NG, x, mem, pos, w)["out"]
    return x.astype(np.float32)
```
